# Optimizing a Trainium2 kernel written in Bass

```python
import jax, jax.numpy as jnp
from jax import lax
import numpy as np

D_MODEL = 1024
BATCH = 8
SEQ = 2048
DEPTH = 4

GRID_W = 64
CTX_LEN = 256
BLOCK = 128
WINDOW = 128
ROPE_THETA = 10000.0
EPS = 1e-6
NEG = -1e30

A_HEADS = 8
A_KV = 2
A_HD = 128
B_HEADS = 8
B_KV = 2
B_HD = 128
C_HEADS = 4
C_DK = 256
C_DV = 512
D_FF = 4 * D_MODEL
N_MOD = 6

A_Q = A_HEADS * A_HD
A_KVW = A_KV * A_HD
B_Q = B_HEADS * B_HD
B_KVW = B_KV * B_HD
C_QK = C_HEADS * C_DK
C_V = C_HEADS * C_DV
IN_SIZES = (A_Q, A_KVW, A_KVW, B_Q, B_KVW, B_KVW, C_QK, C_QK, C_V, C_V, D_MODEL, D_MODEL, D_MODEL)
IN_WIDTH = sum(IN_SIZES)

kernel_name = "hybrid_parallel_gqa_window_retention_dit"


def rmsnorm(x, g):
    xf = x.astype(jnp.float32)
    y = xf * lax.rsqrt(jnp.mean(xf * xf, axis=-1, keepdims=True) + EPS)
    return (y * g).astype(x.dtype)


def split_cols(u):
    offs = np.cumsum(IN_SIZES)[:-1].tolist()
    return jnp.split(u, offs, axis=-1)


def heads(x, n, d):
    return x.reshape(x.shape[:-1] + (n, d))


def group(q, n_kv):
    return q.reshape(q.shape[:2] + (n_kv, q.shape[2] // n_kv, q.shape[3]))


def axial_rope(L, dim):
    rows = L // GRID_W
    row = jnp.repeat(jnp.arange(rows), GRID_W).astype(jnp.float32)
    col = jnp.tile(jnp.arange(GRID_W), rows).astype(jnp.float32)
    n = dim // 4
    inv = ROPE_THETA ** (-jnp.arange(n, dtype=jnp.float32) / n)
    ang = jnp.concatenate([row[:, None] * inv, col[:, None] * inv], axis=-1)
    return jnp.cos(ang)[:, None, :], jnp.sin(ang)[:, None, :]


def rope(x, cs):
    cos, sin = cs
    half = x.shape[-1] // 2
    x1, x2 = x[..., :half], x[..., half:]
    out = jnp.concatenate([x1 * cos - x2 * sin, x1 * sin + x2 * cos], axis=-1)
    return out.astype(x.dtype)


def attend(q, k, v, sink=None, mask=None):
    s = jnp.einsum('bqkgd,bskd->bkgqs', q, k).astype(jnp.float32) * (q.shape[-1] ** -0.5)
    if mask is not None:
        s = jnp.where(mask, s, NEG)
    if sink is not None:
        col = jnp.broadcast_to(sink.astype(jnp.float32)[None, :, :, None, None], s.shape[:-1] + (1,))
        p = jax.nn.softmax(jnp.concatenate([s, col], axis=-1), axis=-1)[..., :-1]
    else:
        p = jax.nn.softmax(s, axis=-1)
    return jnp.einsum('bkgqs,bskd->bqkgd', p.astype(v.dtype), v)


def flat(o):
    return o.reshape(o.shape[:2] + (-1,))


def merge_blocks(o):
    o = jnp.moveaxis(o, 0, 1)
    return o.reshape((o.shape[0], o.shape[1] * o.shape[2], -1))


def mixer_a(q, k, v, qc, kc, vc, qn_g, kn_g, cs, need_ctx):
    L = q.shape[1]
    q = group(rope(rmsnorm(heads(q, A_HEADS, A_HD), qn_g), cs), A_KV)
    k = rope(rmsnorm(heads(k, A_KV, A_HD), kn_g), cs)
    v = heads(v, A_KV, A_HD)
    qc = group(rmsnorm(heads(qc, A_HEADS, A_HD), qn_g), A_KV)
    kc = rmsnorm(heads(kc, A_KV, A_HD), kn_g)
    vc = heads(vc, A_KV, A_HD)
    k_all = jnp.concatenate([k, kc], axis=1)
    v_all = jnp.concatenate([v, vc], axis=1)

    def block(i):
        qi = lax.dynamic_slice_in_dim(q, i * BLOCK, BLOCK, axis=1)
        return attend(qi, k_all, v_all)

    y = merge_blocks(lax.map(block, jnp.arange(L // BLOCK)))
    yc = flat(attend(qc, kc, vc)) if need_ctx else None
    return y, yc


def mixer_b(q, k, v, qc, kc, vc, sink, cs, need_ctx):
    L = q.shape[1]
    q = group(rope(heads(q, B_HEADS, B_HD), cs), B_KV)
    k = rope(heads(k, B_KV, B_HD), cs)
    v = heads(v, B_KV, B_HD)
    qc = group(heads(qc, B_HEADS, B_HD), B_KV)
    kc = heads(kc, B_KV, B_HD)
    vc = heads(vc, B_KV, B_HD)
    snk = sink.reshape(B_KV, B_HEADS // B_KV)
    pad = [(0, 0), (BLOCK, BLOCK), (0, 0), (0, 0)]
    kp = jnp.pad(k, pad)
    vp = jnp.pad(v, pad)
    a = jnp.arange(BLOCK)
    j = jnp.arange(3 * BLOCK)
    band = jnp.abs(a[:, None] + BLOCK - j[None, :]) <= WINDOW
    ctx_ok = jnp.ones((BLOCK, kc.shape[1]), dtype=bool)

    def block(i):
        qi = lax.dynamic_slice_in_dim(q, i * BLOCK, BLOCK, axis=1)
        kw = jnp.concatenate([lax.dynamic_slice_in_dim(kp, i * BLOCK, 3 * BLOCK, axis=1), kc], axis=1)
        vw = jnp.concatenate([lax.dynamic_slice_in_dim(vp, i * BLOCK, 3 * BLOCK, axis=1), vc], axis=1)
        pos = i * BLOCK + j - BLOCK
        valid = band & ((pos >= 0) & (pos < L))[None, :]
        m = jnp.concatenate([valid, ctx_ok], axis=1)
        return attend(qi, kw, vw, snk, m)

    y = merge_blocks(lax.map(block, jnp.arange(L // BLOCK)))
    yc = flat(attend(qc, kc, vc, snk)) if need_ctx else None
    return y, yc


def retention(q, k, v, log_gamma, state):
    bsz, L, H, _ = q.shape
    dv = v.shape[-1]
    n = L // BLOCK
    q, k, v = [t.astype(jnp.float32).reshape(bsz, n, BLOCK, H, -1).swapaxes(0, 1) for t in (q, k, v)]
    idx = jnp.arange(BLOCK, dtype=jnp.float32)
    diff = idx[:, None] - idx[None, :]
    decay_in = jnp.where(diff >= 0, jnp.exp(jnp.maximum(diff, 0.0)[None] * log_gamma[:, None, None]), 0.0)
    q_dec = jnp.exp((idx[:, None] + 1.0) * log_gamma[None, :])
    k_dec = jnp.exp((BLOCK - 1.0 - idx)[:, None] * log_gamma[None, :])
    c_dec = jnp.exp(BLOCK * log_gamma)

    def step(S, xs):
        qb, kb, vb = xs
        att = jnp.einsum('bqhd,bshd->bhqs', qb, kb) * decay_in
        o = jnp.einsum('bhqs,bshe->bqhe', att, vb) + jnp.einsum('bqhd,bhde->bqhe', qb * q_dec[:, :, None], S)
        S = S * c_dec[:, None, None] + jnp.einsum('bshd,bshe->bhde', kb * k_dec[:, :, None], vb)
        return S, o

    S, o = lax.scan(step, state, (q, k, v))
    return o.swapaxes(0, 1).reshape(bsz, L, H, dv), S


def mixer_c(q, k, v, g, qc, kc, vc, gc, logit, norm_g, cs, need_ctx):
    scale = C_DK ** -0.5
    q = rope(heads(q, C_HEADS, C_DK), cs)
    k = rope(heads(k, C_HEADS, C_DK), cs) * scale
    v = heads(v, C_HEADS, C_DV)
    qc = heads(qc, C_HEADS, C_DK)
    kc = heads(kc, C_HEADS, C_DK) * scale
    vc = heads(vc, C_HEADS, C_DV)
    lg = jax.nn.log_sigmoid(logit.astype(jnp.float32))
    zero = jnp.zeros((q.shape[0], C_HEADS, C_DK, C_DV), jnp.float32)
    fl = lambda t: jnp.flip(t, axis=1)
    oc_f, s_f = retention(qc, kc, vc, lg[0], zero)
    oc_b, s_b = retention(fl(qc), fl(kc), fl(vc), lg[1], zero)
    o_f, _ = retention(q, k, v, lg[0], s_f)
    o_b, _ = retention(fl(q), fl(k), fl(v), lg[1], s_b)
    gn = norm_g.reshape(C_HEADS, C_DV)

    def out(o, gate):
        o = rmsnorm(o.astype(gate.dtype), gn)
        return o.reshape(o.shape[:2] + (C_V,)) * jax.nn.silu(gate)

    y = out(o_f + fl(o_b), g)
    yc = out(oc_f + fl(oc_b), gc) if need_ctx else None
    return y, yc


def merge(ya, yb, yc, gates, w_pa, w_pb, w_pc, w_o):
    ga, gb, gc = gates
    y = jax.nn.sigmoid(ga) * (ya @ w_pa) + jax.nn.sigmoid(gb) * (yb @ w_pb) + jax.nn.sigmoid(gc) * (yc @ w_pc)
    return y @ w_o


def mlp(h, w1, w2):
    return jnp.square(jax.nn.relu(h @ w1)) @ w2


def setup_inputs(seed: int = 0) -> dict:
    key = jax.random.key(seed)
    ks = jax.random.split(key, 24)
    f32 = jnp.float32
    nrm = lambda k, shape, fan: jax.random.normal(k, shape, f32) * (fan ** -0.5)
    gain = lambda k, shape: 1.0 + 0.05 * jax.random.normal(k, shape, f32)
    gamma0 = 1.0 - 2.0 ** (-5.0 - jnp.arange(C_HEADS, dtype=f32))
    base_logit = jnp.log(gamma0) - jnp.log1p(-gamma0)
    return {
        "x": jax.random.normal(ks[0], (BATCH, SEQ, D_MODEL), f32),
        "c": jax.random.normal(ks[1], (BATCH, D_MODEL), f32),
        "ctx": jax.random.normal(ks[2], (BATCH, CTX_LEN, D_MODEL), f32),
        "c_ctx": jax.random.normal(ks[3], (D_MODEL,), f32),
        "w_ada": 0.5 * nrm(ks[4], (DEPTH, D_MODEL, N_MOD * D_MODEL), D_MODEL),
        "b_ada": 0.02 * jax.random.normal(ks[5], (DEPTH, N_MOD * D_MODEL), f32),
        "norm1_g": gain(ks[6], (DEPTH, D_MODEL)),
        "norm2_g": gain(ks[7], (DEPTH, D_MODEL)),
        "w_in": nrm(ks[8], (DEPTH, D_MODEL, IN_WIDTH), D_MODEL),
        "qn_g": gain(ks[9], (DEPTH, A_HD)),
        "kn_g": gain(ks[10], (DEPTH, A_HD)),
        "sink": jax.random.normal(ks[11], (DEPTH, B_HEADS), f32),
        "ret_logit": base_logit + 0.1 * jax.random.normal(ks[12], (DEPTH, 2, C_HEADS), f32),
        "ret_norm_g": gain(ks[13], (DEPTH, C_V)),
        "w_pa": nrm(ks[14], (DEPTH, A_Q, D_MODEL), A_Q),
        "w_pb": nrm(ks[15], (DEPTH, B_Q, D_MODEL), B_Q),
        "w_pc": nrm(ks[16], (DEPTH, C_V, D_MODEL), C_V),
        "w_o": nrm(ks[17], (DEPTH, D_MODEL, D_MODEL), D_MODEL),
        "w_ff1": nrm(ks[18], (DEPTH, D_MODEL, D_FF), D_MODEL),
        "w_ff2": nrm(ks[19], (DEPTH, D_FF, D_MODEL), D_FF),
        "final_g": gain(ks[20], (D_MODEL,)),
    }


def reference(x, c, ctx, c_ctx, w_ada, b_ada, norm1_g, norm2_g, w_in, qn_g, kn_g, sink,
              ret_logit, ret_norm_g, w_pa, w_pb, w_pc, w_o, w_ff1, w_ff2, final_g):
    L = x.shape[1]
    cs_h = axial_rope(L, A_HD)
    cs_r = axial_rope(L, C_DK)
    s_lat = jax.nn.silu(c)
    s_ctx = jax.nn.silu(c_ctx)[None]
    xc = ctx
    for l in range(DEPTH):
        need_ctx = l < DEPTH - 1
        mod = (s_lat @ w_ada[l] + b_ada[l])[:, None, :]
        modc = (s_ctx @ w_ada[l] + b_ada[l])[:, None, :]
        sh1, sc1, g1, sh2, sc2, g2 = jnp.split(mod, N_MOD, axis=-1)
        sh1c, sc1c, g1c, sh2c, sc2c, g2c = jnp.split(modc, N_MOD, axis=-1)

        h = rmsnorm(x, norm1_g[l]) * (1.0 + sc1) + sh1
        hc = rmsnorm(xc, norm1_g[l]) * (1.0 + sc1c) + sh1c
        u = split_cols(h @ w_in[l])
        uc = split_cols(hc @ w_in[l])

        ya, yac = mixer_a(u[0], u[1], u[2], uc[0], uc[1], uc[2], qn_g[l], kn_g[l], cs_h, need_ctx)
        yb, ybc = mixer_b(u[3], u[4], u[5], uc[3], uc[4], uc[5], sink[l], cs_h, need_ctx)
        yr, yrc = mixer_c(u[6], u[7], u[8], u[9], uc[6], uc[7], uc[8], uc[9],
                          ret_logit[l], ret_norm_g[l], cs_r, need_ctx)

        x = x + g1 * merge(ya, yb, yr, u[10:13], w_pa[l], w_pb[l], w_pc[l], w_o[l])
        x = x + g2 * mlp(rmsnorm(x, norm2_g[l]) * (1.0 + sc2) + sh2, w_ff1[l], w_ff2[l])
        if need_ctx:
            xc = xc + g1c * merge(yac, ybc, yrc, uc[10:13], w_pa[l], w_pb[l], w_pc[l], w_o[l])
            xc = xc + g2c * mlp(rmsnorm(xc, norm2_g[l]) * (1.0 + sc2c) + sh2c, w_ff1[l], w_ff2[l])
    return rmsnorm(x, final_g)
```

```python
import numpy as np
from contextlib import ExitStack
import concourse.bass as bass
import concourse.mybir as mybir
from concourse.bass_utils import run_bass_kernel_spmd

F32 = mybir.dt.float32
BF16 = mybir.dt.bfloat16
ALU = mybir.AluOpType
AF = mybir.ActivationFunctionType

NDS = 40
EPS = 1e-6
D = 1024
NT = 18
NTOK = NT * 128
NL = 16
DEPTH = 4


class Buf:
    __slots__ = ("w", "r")

    def __init__(self):
        self.w = None
        self.r = []


class V:
    __slots__ = ("ap", "bufs")

    def __init__(self, ap, bufs):
        self.ap = ap
        self.bufs = bufs

    def __getitem__(self, idx):
        return V(self.ap[idx], self.bufs)

    def re(self, pat_, **kw):
        return V(self.ap.rearrange(pat_, **kw), self.bufs)

    def bc(self, axis, n):
        a = self.ap.unsqueeze(axis)
        shp = list(a.shape)
        shp[axis] = n
        return V(a.broadcast_to(shp), self.bufs)

    def pbc(self, n=128):
        return V(self.ap.partition_broadcast(n), self.bufs)


class T:
    def __init__(self, ap, tracked=True):
        self.ap = ap
        self.tracked = tracked
        self.subs = {}
        self.whole = Buf() if tracked else None

    def __getitem__(self, idx):
        return V(self.ap[idx], [self.whole] if self.tracked else [])

    def sub(self, key):
        if key not in self.subs:
            self.subs[key] = Buf()
        return V(self.ap, [self.subs[key]])

    def all(self):
        return V(self.ap, [self.whole] + list(self.subs.values()))

    def re(self, pat_, **kw):
        return V(self.ap.rearrange(pat_, **kw), [self.whole] if self.tracked else [])


class Eng:
    def __init__(self, name, eng, sem):
        self.name = name
        self.eng = eng
        self.sem = sem
        self.count = 0
        self.pending = False
        self.known = {}


WRITE_KW = ("out", "accum_out", "ap")


class K:
    def __init__(self, nc, stack):
        self.nc = nc
        self.stack = stack
        self.engs = {}
        for name, e in (("pe", nc.tensor), ("act", nc.scalar), ("dve", nc.vector),
                        ("pool", nc.gpsimd), ("sp", nc.sync)):
            sem = stack.enter_context(nc.semaphore("s_" + name))
            self.engs[name] = Eng(name, e, sem)
        self.dsems = [stack.enter_context(nc.semaphore("d%d" % i)) for i in range(NDS)]
        self.dcount = 0
        self.dlast = [0] * NDS
        self.ninstr = 0
        self.uid = 0

    def sb(self, shape, dtype, stack=None):
        self.uid += 1
        st = stack or self.stack
        t = st.enter_context(self.nc.sbuf_tensor("t%d" % self.uid, list(shape), dtype))
        return T(t[:])

    def ps(self, shape, dtype, stack=None):
        self.uid += 1
        st = stack or self.stack
        t = st.enter_context(self.nc.psum_tensor("p%d" % self.uid, list(shape), dtype))
        return T(t[:])

    def dram(self, name, shape, dtype, kind="Internal"):
        t = self.nc.dram_tensor(name, list(shape), dtype, kind=kind)
        return T(t.ap(), tracked=False)

    def _wait(self, E, tok):
        if tok[0] == "E":
            _, en, val = tok
            key = en
            sem = self.engs[en].sem
        else:
            _, slot, val = tok
            key = ("D", slot)
            sem = self.dsems[slot]
        if E.known.get(key, 0) >= val:
            return
        E.known[key] = val
        E.eng.wait_ge(sem, val)
        self.ninstr += 1

    def _deps(self, E, reads, writes, is_dma):
        toks = []
        for b in reads:
            if b.w is not None:
                toks.append((b.w, True))
        for b in writes:
            if b.w is not None:
                toks.append((b.w, False))
            for t in b.r:
                toks.append((t, False))
        for tok, raw in toks:
            if tok[0] == "E" and tok[1] == E.name and not is_dma:
                if not raw or E.name == "pe":
                    continue
            self._wait(E, tok)

    def _finish(self, tok, reads, writes):
        for b in writes:
            b.w = tok
            b.r = []
        for b in reads:
            if b in writes:
                continue
            if tok[0] == "E":
                b.r = [t for t in b.r if not (t[0] == "E" and t[1] == tok[1])]
            b.r.append(tok)

    def op(self, en, method, inc=True, **kw):
        E = self.engs[en]
        reads, writes = [], []
        args = {}
        for k_, v in kw.items():
            if isinstance(v, V):
                (writes if k_ in WRITE_KW else reads).extend(v.bufs)
                args[k_] = v.ap
            else:
                args[k_] = v
        self._deps(E, reads, writes, False)
        ins = getattr(E.eng, method)(**args)
        self.ninstr += 1
        if inc:
            E.count += 1
            ins.then_inc(E.sem, 1)
            E.pending = False
            tok = ("E", en, E.count)
        else:
            E.pending = True
            tok = ("E", en, E.count + 1)
        self._finish(tok, reads, writes)
        return ins

    def dma(self, qn, out, in_):
        E = self.engs[qn]
        reads, writes = list(in_.bufs), list(out.bufs)
        self._deps(E, reads, writes, True)
        i = self.dcount
        self.dcount += 1
        slot = i % NDS
        prev = self.dlast[slot]
        if prev:
            self._wait(E, ("D", slot, prev))
        val = prev + 16
        self.dlast[slot] = val
        ins = E.eng.dma_start(out=out.ap, in_=in_.ap)
        ins.then_inc(self.dsems[slot], 16)
        self.ninstr += 1
        tok = ("D", slot, val)
        self._finish(tok, reads, writes)
        return tok

    def barrier(self):
        sp = self.engs["sp"]
        for slot in range(NDS):
            if self.dlast[slot]:
                self._wait(sp, ("D", slot, self.dlast[slot]))
        for en, E in self.engs.items():
            assert not E.pending, en
        sp.count += 1
        sp.eng.sem_inc(sp.sem, 1)
        self.ninstr += 1
        for en, E in self.engs.items():
            for on, O in self.engs.items():
                if on != en and O.count:
                    self._wait(E, ("E", on, O.count))
        for en, E in self.engs.items():
            for slot in range(NDS):
                if self.dlast[slot]:
                    E.known[("D", slot)] = self.dlast[slot]


class Rot:
    def __init__(self, tiles):
        self.tiles = tiles
        self.i = 0

    def next(self):
        t = self.tiles[self.i % len(self.tiles)]
        self.i += 1
        return t


def build(depth=DEPTH, dbg=False):
    nc = bass.Bass("TRN2", target_bir_lowering=False)
    with ExitStack() as top:
        k = K(nc, top)
        ei = lambda n, s: k.dram(n, s, F32, kind="ExternalInput")
        x_in = ei("x_in", [2048, D])
        ctx_in = ei("ctx_in", [256, D])
        ccol = ei("ccol", [128, 8, 2])
        w_ada = ei("w_ada", [DEPTH, D, 6 * D])
        b_ada = ei("b_ada", [DEPTH, 6 * D])
        norm1_g = ei("norm1_g", [DEPTH, D])
        norm2_g = ei("norm2_g", [DEPTH, D])
        w_in = ei("w_in", [DEPTH, D, 12288])
        qn_g = ei("qn_g", [DEPTH, 128])
        kn_g = ei("kn_g", [DEPTH, 128])
        sink = ei("sink", [DEPTH, 8])
        ret_logit = ei("ret_logit", [DEPTH, 8])
        ret_norm_g = ei("ret_norm_g", [DEPTH, 2048])
        w_pa = ei("w_pa", [DEPTH, D, D])
        w_pb = ei("w_pb", [DEPTH, D, D])
        w_pc = ei("w_pc", [DEPTH, 2048, D])
        w_o = ei("w_o", [DEPTH, D, D])
        w_ff1 = ei("w_ff1", [DEPTH, D, 4096])
        w_ff2 = ei("w_ff2", [DEPTH, 4096, D])
        final_g = ei("final_g", [D])
        c_ident = ei("c_ident", [128, 128])
        c_ropeh = ei("c_ropeh", [2048, 2, 128])
        c_roper = ei("c_roper", [2048, 2, 256])
        c_mask = ei("c_mask", [128, 2, 128])
        c_ret = ei("c_ret", [128, 4, 128])
        c_ecol = ei("c_ecol", [128, 4])
        out = k.dram("out", [2048, D], F32, kind="ExternalOutput")
        dbg_out = {}

        xres = k.dram("xres", [NTOK, D], F32)
        mods = k.dram("mods", [DEPTH, 2, 6 * D], F32)
        AqT = k.dram("AqT", [8, 128, NTOK], BF16)
        AkT = k.dram("AkT", [2, 128, NTOK], BF16)
        Av = k.dram("Av", [NTOK, 256], BF16)
        BqT = k.dram("BqT", [2, 128, NT, 4, 128], BF16)
        BkT = k.dram("BkT", [2, 128, NTOK], BF16)
        Bv = k.dram("Bv", [NTOK, 256], BF16)
        CqT = k.dram("CqT", [4, 2, 128, NTOK], BF16)
        CkT = k.dram("CkT", [4, 2, 128, NTOK], BF16)
        Ck = k.dram("Ck", [NTOK, 1024], BF16)
        Cv = k.dram("Cv", [NTOK, 2048], BF16)
        Cg = k.dram("Cg", [NTOK, 2048], BF16)
        Gt = k.dram("Gt", [NTOK, 3072], BF16)
        yaT = k.dram("yaT", [1024, NTOK], BF16)
        ybT = k.dram("ybT", [1024, NTOK], BF16)
        ycT = k.dram("ycT", [2048, NTOK], BF16)
        hidT = k.dram("hidT", [4096, NTOK], BF16)

        ident = k.sb([128, 128], BF16)
        ones = k.sb([128, 128], BF16)
        maskb = k.sb([128, 2, 128], BF16)
        cret = k.sb([128, 4, 128], F32)
        ecol = k.sb([128, 4], F32)
        k.dma("pool", ident[:], c_ident[:, :])
        k.dma("pool", maskb[:], c_mask[:, :, :])
        k.dma("sp", cret[:], c_ret[:, :, :])
        k.dma("sp", ecol[:], c_ecol[:, :])
        k.op("dve", "memset", ap=ones[:], constant=1.0)
        k.dma("sp", xres[0:2048, :], x_in[:, :])
        k.dma("sp", xres[2048:NTOK, :], ctx_in[:, :])

        def rsqrt_mean(dst, src, n):
            k.op("dve", "tensor_scalar", out=dst, in0=src, scalar1=1.0 / n, scalar2=EPS,
                 op0=ALU.mult, op1=ALU.add)
            k.op("act", "activation", out=dst, in_=dst, func=AF.Sqrt)
            k.op("dve", "reciprocal", out=dst, in_=dst)

        with ExitStack() as st:
            cs = k.sb([128, 8, 2], F32, st)
            csb = k.sb([128, 8, 2], BF16, st)
            k.dma("sp", cs[:], ccol[:, :, :])
            k.op("act", "activation", out=csb[:], in_=cs[:], func=AF.Silu)
            wa = Rot([k.sb([128, 8, 1536], BF16, st) for _ in range(2)])
            bia = Rot([k.sb([2, 1536], F32, st) for _ in range(2)])
            mo = Rot([k.sb([2, 1536], F32, st) for _ in range(2)])
            pm = Rot([k.ps([2, 512], F32, st) for _ in range(3)])
            for l in range(depth):
                for ch in range(4):
                    w = wa.next()
                    k.dma("pool", w[:], w_ada[l, :, ch * 1536:(ch + 1) * 1536].re("(kc p) n -> p kc n", p=128))
                    bt = bia.next()
                    k.dma("sp", bt[:], b_ada[l, ch * 1536:(ch + 1) * 1536].pbc(2))
                    m = mo.next()
                    for j in range(3):
                        p = pm.next()
                        for kc in range(8):
                            k.op("pe", "matmul", out=p[:], lhsT=csb[:, kc, :], rhs=w[:, kc, j * 512:(j + 1) * 512],
                                 start=(kc == 0), stop=(kc == 7), inc=(kc == 7))
                        k.op("dve", "tensor_tensor", out=m[:, j * 512:(j + 1) * 512], in0=p[:],
                             in1=bt[:, j * 512:(j + 1) * 512], op=ALU.add)
                    k.dma("sp", mods[l, :, ch * 1536:(ch + 1) * 1536], m[:])
            k.barrier()

        def norm_stage(l, gvec, i_sh, i_sc, hT, tiles):
            with ExitStack() as st:
                gt = k.sb([128, D], F32, st)
                G = [k.sb([128, D], F32, st) for _ in range(2)]
                SH = [k.sb([128, D], F32, st) for _ in range(2)]
                k.dma("sp", gt[:], gvec[l, :].pbc())
                for r in range(2):
                    k.dma("sp", G[r][:], mods[l, r, i_sc * D:(i_sc + 1) * D].pbc())
                    k.dma("sp", SH[r][:], mods[l, r, i_sh * D:(i_sh + 1) * D].pbc())
                    k.op("dve", "scalar_tensor_tensor", out=G[r][:], in0=G[r][:], scalar=1.0, in1=gt[:],
                         op0=ALU.add, op1=ALU.mult)
                xt = Rot([k.sb([128, D], F32, st) for _ in range(3)])
                sq = k.sb([128, D], F32, st)
                ssr = Rot([k.sb([128, 1], F32, st) for _ in range(3)])
                h1 = Rot([k.sb([128, D], F32, st) for _ in range(2)])
                hb = Rot([k.sb([128, D], BF16, st) for _ in range(2)])
                pT = Rot([k.ps([128, 8, 128], BF16, st) for _ in range(2)])
                for t in tiles:
                    r = 0 if t < NL else 1
                    x = xt.next()
                    k.dma("sp", x[:], xres[t * 128:(t + 1) * 128, :])
                    ss = ssr.next()
                    k.op("act", "activation", out=sq[:], in_=x[:], func=AF.Square, accum_out=ss[:])
                    rsqrt_mean(ss[:], ss[:], D)
                    a = h1.next()
                    k.op("dve", "scalar_tensor_tensor", out=a[:], in0=x[:], scalar=ss[:], in1=G[r][:],
                         op0=ALU.mult, op1=ALU.mult)
                    b = hb.next()
                    k.op("dve", "tensor_tensor", out=b[:], in0=a[:], in1=SH[r][:], op=ALU.add)
                    p = pT.next()
                    for kc in range(8):
                        k.op("pe", "transpose", out=p[:, kc, :], in_=b[:, kc * 128:(kc + 1) * 128],
                             identity=ident[:], inc=(kc == 7))
                    k.op("act", "copy", out=hT.sub(t)[:, :, t * 128:(t + 1) * 128], in_=p[:])
                k.barrier()

        def proj_stage(l, hT, tiles):
            with ExitStack() as st:
                wp = Rot([k.sb([128, 8, 512], BF16, st) for _ in range(2)])
                rh = k.sb([128, NL, 2, 128], F32, st)
                rr = k.sb([128, NL, 2, 256], F32, st)
                k.dma("sp", rh[:], c_ropeh.re("(t p) c d -> p t c d", p=128))
                k.dma("sp", rr[:], c_roper.re("(t p) c d -> p t c d", p=128))
                gq = k.sb([128, 128], F32, st)
                gk = k.sb([128, 128], F32, st)
                k.dma("sp", gq[:], qn_g[l, :].pbc())
                k.dma("sp", gk[:], kn_g[l, :].pbc())
                k.op("dve", "tensor_scalar", out=gq[:], in0=gq[:], scalar1=128.0 ** -0.5, scalar2=None, op0=ALU.mult)
                pu = Rot([k.ps([128, 512], F32, st) for _ in range(3)])
                pT = Rot([k.ps([128, 4, 128], BF16, st) for _ in range(2)])
                U = Rot([k.sb([128, 512], F32, st) for _ in range(2)])
                T1 = Rot([k.sb([128, 512], F32, st) for _ in range(2)])
                T2 = Rot([k.sb([128, 512], F32, st) for _ in range(2)])
                Rb = Rot([k.sb([128, 512], BF16, st) for _ in range(3)])
                RT = Rot([k.sb([128, 4, 128], BF16, st) for _ in range(3)])
                sq = k.sb([128, 128], F32, st)
                ssr = Rot([k.sb([128, 4], F32, st) for _ in range(3)])

                def rope(u, nh, half, tab, t, ob):
                    w = nh * 2 * half
                    u3 = u.re("p (h c d) -> p h c d", h=nh, c=2)
                    t1 = T1.next()
                    t2 = T2.next()
                    cosv = tab[:, t, 0, :].bc(1, nh)
                    k.op("dve", "tensor_tensor", out=t1[:, 0:w].re("p (h d) -> p h d", h=nh),
                         in0=u.re("p (h d) -> p h d", h=nh), in1=cosv, op=ALU.mult)
                    t23 = t2[:, 0:w].re("p (h c d) -> p h c d", h=nh, c=2)
                    sin3 = tab[:, t, 1, :].re("p (c d) -> p c d", c=2)
                    k.op("dve", "tensor_tensor", out=t23[:, :, 0, :], in0=u3[:, :, 1, :],
                         in1=sin3[:, 0, :].bc(1, nh), op=ALU.mult)
                    k.op("dve", "tensor_tensor", out=t23[:, :, 1, :], in0=u3[:, :, 0, :],
                         in1=sin3[:, 1, :].bc(1, nh), op=ALU.mult)
                    k.op("dve", "tensor_tensor", out=ob, in0=t1[:, 0:w], in1=t2[:, 0:w], op=ALU.add)

                def transposes(rb, n):
                    p = pT.next()
                    for j in range(n):
                        k.op("pe", "transpose", out=p[:, j, :], in_=rb[:, j * 128:(j + 1) * 128],
                             identity=ident[:], inc=(j == n - 1))
                    rt = RT.next()
                    k.op("act", "copy", out=rt[:, 0:n, :], in_=p[:, 0:n, :])
                    return rt

                def qknorm(ps, nh, gvec):
                    ss = ssr.next()
                    for j in range(nh):
                        k.op("act", "activation", out=sq[:], in_=ps[:, j * 128:(j + 1) * 128], func=AF.Square,
                             accum_out=ss[:, j:j + 1])
                    rsqrt_mean(ss[:, 0:nh], ss[:, 0:nh], 128)
                    u = U.next()
                    for j in range(nh):
                        k.op("dve", "scalar_tensor_tensor", out=u[:, j * 128:(j + 1) * 128],
                             in0=ps[:, j * 128:(j + 1) * 128], scalar=ss[:, j:j + 1], in1=gvec[:],
                             op0=ALU.mult, op1=ALU.mult)
                    return u

                for p_i in range(24):
                    w = wp.next()
                    k.dma("pool", w[:], w_in[l, :, p_i * 512:(p_i + 1) * 512].re("(kc p) n -> p kc n", p=128))
                    for t in tiles:
                        lat = t < NL
                        tok = slice(t * 128, (t + 1) * 128)
                        ps = pu.next()
                        for kc in range(8):
                            k.op("pe", "matmul", out=ps[:], lhsT=hT.sub(t)[:, kc, tok], rhs=w[:, kc, :],
                                 start=(kc == 0), stop=(kc == 7), inc=(kc == 7))
                        if p_i in (0, 1):
                            u = qknorm(ps[:], 4, gq)
                            rb = Rb.next()
                            if lat:
                                rope(u[:], 4, 64, rh, t, rb[:])
                            else:
                                k.op("act", "copy", out=rb[:], in_=u[:])
                            rt = transposes(rb, 4)
                            k.dma("sp", AqT[4 * p_i:4 * p_i + 4, :, tok].re("h d q -> d h q"), rt[:])
                        elif p_i in (2, 5):
                            rb = Rb.next()
                            if p_i == 2:
                                u = qknorm(ps[:, 0:256], 2, gk)
                            else:
                                u = U.next()
                                k.op("act", "copy", out=u[:, 0:256], in_=ps[:, 0:256])
                            if lat:
                                rope(u[:, 0:256], 2, 64, rh, t, rb[:, 0:256])
                            else:
                                k.op("act", "copy", out=rb[:, 0:256], in_=u[:, 0:256])
                            k.op("act", "copy", out=rb[:, 256:512], in_=ps[:, 256:512])
                            rt = transposes(rb, 2)
                            dk_, dv_ = (AkT, Av) if p_i == 2 else (BkT, Bv)
                            k.dma("sp", dk_[:, :, tok].re("h d q -> d h q"), rt[:, 0:2, :])
                            k.dma("sp", dv_[tok, :], rb[:, 256:512])
                        elif p_i in (3, 4):
                            u = U.next()
                            k.op("act", "activation", out=u[:], in_=ps[:], func=AF.Copy, scale=128.0 ** -0.5)
                            rb = Rb.next()
                            if lat:
                                rope(u[:], 4, 64, rh, t, rb[:])
                            else:
                                k.op("act", "copy", out=rb[:], in_=u[:])
                            rt = transposes(rb, 4)
                            k.dma("sp", BqT[p_i - 3, :, t, :, :], rt[:])
                        elif p_i in (6, 7, 8, 9):
                            u = U.next()
                            isk = p_i >= 8
                            k.op("act", "activation", out=u[:], in_=ps[:], func=AF.Copy,
                                 scale=(256.0 ** -0.5 if isk else 1.0))
                            rb = Rb.next()
                            if lat:
                                rope(u[:], 2, 128, rr, t, rb[:])
                            else:
                                k.op("act", "copy", out=rb[:], in_=u[:])
                            rt = transposes(rb, 4)
                            h0 = 2 * ((p_i - 6) % 2)
                            dst = CkT if isk else CqT
                            k.dma("sp", dst[h0:h0 + 2, :, :, tok].re("h c d q -> d (h c) q"), rt[:])
                            if isk:
                                k.dma("sp", Ck[tok, h0 * 256:(h0 + 2) * 256], rb[:])
                        else:
                            rb = Rb.next()
                            if p_i < 14:
                                k.op("act", "copy", out=rb[:], in_=ps[:])
                                k.dma("sp", Cv[tok, (p_i - 10) * 512:(p_i - 9) * 512], rb[:])
                            elif p_i < 18:
                                k.op("act", "activation", out=rb[:], in_=ps[:], func=AF.Silu)
                                k.dma("sp", Cg[tok, (p_i - 14) * 512:(p_i - 13) * 512], rb[:])
                            else:
                                k.op("act", "activation", out=rb[:], in_=ps[:], func=AF.Sigmoid)
                                k.dma("sp", Gt[tok, (p_i - 18) * 512:(p_i - 17) * 512], rb[:])
                k.barrier()

        def attend(res, qv, n, keys, sinkv):
            acc_o = res["acc_o"].next()
            acc_s = res["acc_s"].next()
            nk = len(keys)
            for i, (kv, vv, mv) in enumerate(keys):
                s = res["s"].next()
                k.op("pe", "matmul", out=s[:, 0:n], lhsT=kv, rhs=qv, start=True, stop=True)
                pt = res["pt"].next()
                k.op("act", "activation", out=pt[:, 0:n], in_=s[:, 0:n], func=AF.Exp)
                if mv is not None:
                    k.op("dve", "tensor_tensor", out=pt[:, 0:n].re("p (h q) -> p h q", q=128),
                         in0=pt[:, 0:n].re("p (h q) -> p h q", q=128), in1=mv.bc(1, n // 128), op=ALU.mult)
                k.op("pe", "matmul", out=acc_o[:, 0:n], lhsT=vv, rhs=pt[:, 0:n], start=(i == 0), stop=(i == nk - 1),
                     inc=False)
                k.op("pe", "matmul", out=acc_s[:, 0:n], lhsT=ones[:], rhs=pt[:, 0:n], start=(i == 0),
                     stop=(i == nk - 1), inc=True)
            rec = res["rec"].next()
            if sinkv is not None:
                k.op("dve", "tensor_tensor", out=rec[:, 0:n].re("p (h q) -> p h q", q=128),
                     in0=acc_s[:, 0:n].re("p (h q) -> p h q", q=128), in1=sinkv.bc(2, 128), op=ALU.add)
                k.op("dve", "reciprocal", out=rec[:, 0:n], in_=rec[:, 0:n])
            else:
                k.op("dve", "reciprocal", out=rec[:, 0:n], in_=acc_s[:, 0:n])
            o = res["o"].next()
            k.op("dve", "tensor_tensor", out=o[:, 0:n], in0=acc_o[:, 0:n], in1=rec[:, 0:n], op=ALU.mult)
            return o

        def attn_stage(l, need_ctx):
            with ExitStack() as st:
                res = dict(
                    s=Rot([k.ps([128, 512], F32, st) for _ in range(2)]),
                    acc_o=Rot([k.ps([128, 512], F32, st) for _ in range(2)]),
                    acc_s=Rot([k.ps([128, 512], F32, st) for _ in range(2)]),
                    pt=Rot([k.sb([128, 512], BF16, st) for _ in range(3)]),
                    rec=Rot([k.sb([128, 512], F32, st) for _ in range(2)]),
                    o=Rot([k.sb([128, 512], BF16, st) for _ in range(3)]),
                )
                kTt = Rot([k.sb([128, NTOK], BF16, st) for _ in range(2)])
                vt = Rot([k.sb([128, NT, 128], BF16, st) for _ in range(2)])
                qt = Rot([k.sb([128, 512], BF16, st) for _ in range(3)])
                snk = k.sb([128, 8], F32, st)
                k.dma("sp", snk[:], sink[l, :].pbc())
                k.op("act", "activation", out=snk[:], in_=snk[:], func=AF.Exp)
                for g in range(2):
                    kT = kTt.next()
                    vv = vt.next()
                    k.dma("sp", kT[:], AkT[g, :, :])
                    k.dma("sp", vv[:], Av[:, g * 128:(g + 1) * 128].re("(c s) d -> s c d", s=128))
                    for h in range(4):
                        hd = 4 * g + h
                        for qc in range(5 if need_ctx else 4):
                            n = 512 if qc < 4 else 256
                            q0 = qc * 512
                            q = qt.next()
                            k.dma("sp", q[:, 0:n], AqT[hd, :, q0:q0 + n])
                            cl = range(NT) if qc < 4 else (16, 17)
                            keys = [(kT[:, c * 128:(c + 1) * 128], vv[:, c, :], None) for c in cl]
                            o = attend(res, q[:, 0:n], n, keys, None)
                            k.dma("pool", yaT[hd * 128:(hd + 1) * 128, q0:q0 + n], o[:, 0:n])
                for g in range(2):
                    kT = kTt.next()
                    vv = vt.next()
                    k.dma("sp", kT[:], BkT[g, :, :])
                    k.dma("sp", vv[:], Bv[:, g * 128:(g + 1) * 128].re("(c s) d -> s c d", s=128))
                    for blk in range(NT if need_ctx else NL):
                        q = qt.next()
                        k.dma("sp", q[:], BqT[g, :, blk, :, :].re("d h q -> d (h q)"))
                        if blk < NL:
                            cl = []
                            if blk > 0:
                                cl.append((blk - 1, maskb[:, 0, :]))
                            cl.append((blk, None))
                            if blk < NL - 1:
                                cl.append((blk + 1, maskb[:, 1, :]))
                            cl += [(16, None), (17, None)]
                        else:
                            cl = [(16, None), (17, None)]
                        keys = [(kT[:, c * 128:(c + 1) * 128], vv[:, c, :], m) for c, m in cl]
                        o = attend(res, q[:], 512, keys, snk[:, 4 * g:4 * g + 4])
                        k.dma("pool", ybT[g * 512:(g + 1) * 512, blk * 128:(blk + 1) * 128].re("(h d) q -> d h q", h=4),
                              o[:].re("d (h q) -> d h q", h=4))
                k.barrier()

        def ret_stage(l, need_ctx):
            with ExitStack() as st:
                lg = k.sb([128, 8], F32, st)
                k.dma("sp", lg[:], ret_logit[l, :].pbc())
                k.op("act", "activation", out=lg[:], in_=lg[:], func=AF.Exp, scale=-1.0)
                k.op("dve", "tensor_scalar", out=lg[:], in0=lg[:], scalar1=1.0, scalar2=None, op0=ALU.add)
                k.op("act", "activation", out=lg[:], in_=lg[:], func=AF.Ln)
                k.op("dve", "tensor_scalar", out=lg[:], in0=lg[:], scalar1=-1.0, scalar2=None, op0=ALU.mult)
                gn = k.sb([128, 2048], F32, st)
                k.dma("sp", gn[:], ret_norm_g[l, :].pbc())
                dec = k.sb([128, 6], F32, st)
                DT = k.sb([128, 128], F32, st)
                DT2 = k.sb([128, 128], F32, st)
                qT = k.sb([128, 2, NTOK], BF16, st)
                kT = k.sb([128, 2, NTOK], BF16, st)
                ktok = k.sb([128, NT, 256], BF16, st)
                vtok = k.sb([128, NT, 512], BF16, st)
                OB = k.sb([128, NT, 512], F32, st)
                S = [k.sb([128, 2, 512], F32, st) for _ in range(2)]
                Sb = [k.sb([128, 2, 512], BF16, st) for _ in range(2)]
                Kd = Rot([k.sb([128, 256], BF16, st) for _ in range(2)])
                attb = Rot([k.sb([128, 128], BF16, st) for _ in range(2)])
                tmp = Rot([k.sb([128, 512], F32, st) for _ in range(2)])
                o1 = Rot([k.sb([128, 512], F32, st) for _ in range(2)])
                sq = k.sb([128, 512], F32, st)
                ssr = Rot([k.sb([128, 1], F32, st) for _ in range(2)])
                gate = Rot([k.sb([128, 512], BF16, st) for _ in range(2)])
                yb = Rot([k.sb([128, 512], BF16, st) for _ in range(2)])
                yT = Rot([k.sb([128, 4, 128], BF16, st) for _ in range(2)])
                pbig = Rot([k.ps([128, 512], F32, st) for _ in range(4)])
                psA = Rot([k.ps([128, 128], F32, st) for _ in range(2)])
                pT = Rot([k.ps([128, 4, 128], BF16, st) for _ in range(2)])

                def state_update(d, c, ivec, icd):
                    kd = Kd.next()
                    k.op("dve", "tensor_scalar", out=kd[:], in0=ktok[:, c, :], scalar1=dec[:, ivec:ivec + 1],
                         scalar2=None, op0=ALU.mult)
                    for dc in range(2):
                        p = pbig.next()
                        k.op("pe", "matmul", out=p[:], lhsT=kd[:, dc * 128:(dc + 1) * 128], rhs=vtok[:, c, :],
                             start=True, stop=True)
                        k.op("dve", "scalar_tensor_tensor", out=S[d][:, dc, :], in0=S[d][:, dc, :],
                             scalar=dec[:, icd:icd + 1], in1=p[:], op0=ALU.mult, op1=ALU.add)
                    k.op("act", "copy", out=Sb[d][:], in_=S[d][:])

                def qS(d, c):
                    p = pbig.next()
                    for dc in range(2):
                        k.op("pe", "matmul", out=p[:], lhsT=qT[:, dc, c * 128:(c + 1) * 128], rhs=Sb[d][:, dc, :],
                             start=(dc == 0), stop=(dc == 1), inc=(dc == 1))
                    return p

                def ret_pass(h, chunks, want_out):
                    for c in reversed(chunks):
                        if want_out:
                            p = qS(1, c)
                            k.op("act", "activation", out=OB[:, c, :], in_=p[:], func=AF.Copy, scale=dec[:, 3:4])
                        state_update(1, c, 1, 5)
                    for c in chunks:
                        tok = slice(c * 128, (c + 1) * 128)
                        if want_out:
                            pa = psA.next()
                            for dc in range(2):
                                k.op("pe", "matmul", out=pa[:], lhsT=kT[:, dc, tok], rhs=qT[:, dc, tok],
                                     start=(dc == 0), stop=(dc == 1), inc=(dc == 1))
                            ab = attb.next()
                            k.op("dve", "tensor_tensor", out=ab[:], in0=pa[:], in1=DT[:], op=ALU.mult)
                            p1 = pbig.next()
                            k.op("pe", "matmul", out=p1[:], lhsT=ab[:], rhs=vtok[:, c, :], start=True, stop=True)
                            p2 = qS(0, c)
                            tm = tmp.next()
                            k.op("act", "activation", out=tm[:], in_=p2[:], func=AF.Copy, scale=dec[:, 2:3])
                            o = o1.next()
                            k.op("dve", "tensor_tensor", out=o[:], in0=p1[:], in1=tm[:], op=ALU.add)
                            k.op("dve", "tensor_tensor", out=o[:], in0=o[:], in1=OB[:, c, :], op=ALU.add)
                            ss = ssr.next()
                            k.op("act", "activation", out=sq[:], in_=o[:], func=AF.Square, accum_out=ss[:])
                            rsqrt_mean(ss[:], ss[:], 512)
                            gt = gate.next()
                            k.dma("sp", gt[:], Cg[tok, h * 512:(h + 1) * 512])
                            k.op("dve", "scalar_tensor_tensor", out=o[:], in0=o[:], scalar=ss[:],
                                 in1=gn[:, h * 512:(h + 1) * 512], op0=ALU.mult, op1=ALU.mult)
                            y = yb.next()
                            k.op("dve", "tensor_tensor", out=y[:], in0=o[:], in1=gt[:], op=ALU.mult)
                            p = pT.next()
                            for j in range(4):
                                k.op("pe", "transpose", out=p[:, j, :], in_=y[:, j * 128:(j + 1) * 128],
                                     identity=ident[:], inc=(j == 3))
                            yt = yT.next()
                            k.op("act", "copy", out=yt[:], in_=p[:])
                            k.dma("pool", ycT[h * 512:(h + 1) * 512, tok].re("(j d) q -> d j q", j=4), yt[:])
                        state_update(0, c, 0, 4)

                for h in range(4):
                    k.dma("sp", qT[:], CqT[h, :, :, :].re("c d q -> d c q"))
                    k.dma("sp", kT[:], CkT[h, :, :, :].re("c d q -> d c q"))
                    k.dma("sp", ktok[:], Ck[:, h * 256:(h + 1) * 256].re("(c s) d -> s c d", s=128))
                    k.dma("sp", vtok[:], Cv[:, h * 512:(h + 1) * 512].re("(c s) d -> s c d", s=128))
                    lf = lg[:, h:h + 1]
                    lb = lg[:, 4 + h:5 + h]
                    k.op("act", "activation", out=dec[:, 0:1], in_=ecol[:, 0:1], func=AF.Exp, scale=lf)
                    k.op("act", "activation", out=dec[:, 1:2], in_=ecol[:, 1:2], func=AF.Exp, scale=lb)
                    k.op("act", "activation", out=dec[:, 2:3], in_=ecol[:, 2:3], func=AF.Exp, scale=lf)
                    k.op("act", "activation", out=dec[:, 3:4], in_=ecol[:, 3:4], func=AF.Exp, scale=lb)
                    k.op("act", "activation", out=dec[:, 4:5], in_=lf, func=AF.Exp, scale=128.0)
                    k.op("act", "activation", out=dec[:, 5:6], in_=lb, func=AF.Exp, scale=128.0)
                    k.op("act", "activation", out=DT[:], in_=cret[:, 0, :], func=AF.Exp, scale=lf)
                    k.op("dve", "tensor_tensor", out=DT[:], in0=DT[:], in1=cret[:, 2, :], op=ALU.mult)
                    k.op("act", "activation", out=DT2[:], in_=cret[:, 1, :], func=AF.Exp, scale=lb)
                    k.op("dve", "tensor_tensor", out=DT2[:], in0=DT2[:], in1=cret[:, 3, :], op=ALU.mult)
                    k.op("dve", "tensor_tensor", out=DT[:], in0=DT[:], in1=DT2[:], op=ALU.add)
                    for d in range(2):
                        k.op("dve", "memset", ap=S[d][:], constant=0.0)
                        k.op("dve", "memset", ap=Sb[d][:], constant=0.0)
                    ret_pass(h, [16, 17], need_ctx)
                    ret_pass(h, list(range(NL)), True)
                k.barrier()

        def merge_stage(l, tiles):
            with ExitStack() as st:
                wpa = k.sb([128, 8, D], BF16, st)
                wpb = k.sb([128, 8, D], BF16, st)
                wpc = k.sb([128, 16, D], BF16, st)
                wo = k.sb([128, 8, D], BF16, st)
                k.dma("pool", wpa[:], w_pa[l, :, :].re("(kc p) n -> p kc n", p=128))
                k.dma("pool", wpb[:], w_pb[l, :, :].re("(kc p) n -> p kc n", p=128))
                k.dma("pool", wpc[:, 0:8, :], w_pc[l, 0:1024, :].re("(kc p) n -> p kc n", p=128))
                k.dma("pool", wpc[:, 8:16, :], w_pc[l, 1024:2048, :].re("(kc p) n -> p kc n", p=128))
                k.dma("pool", wo[:], w_o[l, :, :].re("(kc p) n -> p kc n", p=128))
                G1 = [k.sb([128, D], F32, st) for _ in range(2)]
                for r in range(2):
                    k.dma("sp", G1[r][:], mods[l, r, 2 * D:3 * D].pbc())
                ya = Rot([k.sb([128, 8, 128], BF16, st) for _ in range(2)])
                yb_ = Rot([k.sb([128, 8, 128], BF16, st) for _ in range(2)])
                yc = Rot([k.sb([128, 16, 128], BF16, st) for _ in range(2)])
                gts = Rot([k.sb([128, 3072], BF16, st) for _ in range(2)])
                xt = Rot([k.sb([128, D], F32, st) for _ in range(2)])
                xn = Rot([k.sb([128, D], F32, st) for _ in range(2)])
                acc = Rot([k.sb([128, 512], F32, st) for _ in range(2)])
                t2 = Rot([k.sb([128, 512], F32, st) for _ in range(2)])
                y16 = Rot([k.sb([128, D], BF16, st) for _ in range(2)])
                yT = Rot([k.sb([128, 8, 128], BF16, st) for _ in range(2)])
                pp = Rot([k.ps([128, 512], F32, st) for _ in range(5)])
                pT = Rot([k.ps([128, 8, 128], BF16, st) for _ in range(2)])
                for t in tiles:
                    r = 0 if t < NL else 1
                    tok = slice(t * 128, (t + 1) * 128)
                    a = ya.next()
                    b = yb_.next()
                    c = yc.next()
                    g = gts.next()
                    x = xt.next()
                    k.dma("sp", a[:], yaT[:, tok].re("(fc f) q -> f fc q", f=128))
                    k.dma("sp", b[:], ybT[:, tok].re("(fc f) q -> f fc q", f=128))
                    k.dma("sp", c[:], ycT[:, tok].re("(fc f) q -> f fc q", f=128))
                    k.dma("sp", g[:], Gt[tok, :])
                    k.dma("sp", x[:], xres[tok, :])
                    y = y16.next()
                    for nh in range(2):
                        ns = slice(nh * 512, (nh + 1) * 512)
                        pa, pb, pc = pp.next(), pp.next(), pp.next()
                        for fc in range(8):
                            k.op("pe", "matmul", out=pa[:], lhsT=a[:, fc, :], rhs=wpa[:, fc, ns], start=(fc == 0),
                                 stop=(fc == 7), inc=(fc == 7))
                        for fc in range(8):
                            k.op("pe", "matmul", out=pb[:], lhsT=b[:, fc, :], rhs=wpb[:, fc, ns], start=(fc == 0),
                                 stop=(fc == 7), inc=(fc == 7))
                        for fc in range(16):
                            k.op("pe", "matmul", out=pc[:], lhsT=c[:, fc, :], rhs=wpc[:, fc, ns], start=(fc == 0),
                                 stop=(fc == 15), inc=(fc == 15))
                        ac = acc.next()
                        tt = t2.next()
                        k.op("dve", "tensor_tensor", out=ac[:], in0=pa[:], in1=g[:, nh * 512:(nh + 1) * 512], op=ALU.mult)
                        k.op("dve", "tensor_tensor", out=tt[:], in0=pb[:], in1=g[:, 1024 + nh * 512:1024 + (nh + 1) * 512],
                             op=ALU.mult)
                        k.op("dve", "tensor_tensor", out=ac[:], in0=ac[:], in1=tt[:], op=ALU.add)
                        k.op("dve", "tensor_tensor", out=tt[:], in0=pc[:], in1=g[:, 2048 + nh * 512:2048 + (nh + 1) * 512],
                             op=ALU.mult)
                        k.op("dve", "tensor_tensor", out=y[:, ns], in0=ac[:], in1=tt[:], op=ALU.add)
                    p = pT.next()
                    for fc in range(8):
                        k.op("pe", "transpose", out=p[:, fc, :], in_=y[:, fc * 128:(fc + 1) * 128], identity=ident[:],
                             inc=(fc == 7))
                    yt = yT.next()
                    k.op("act", "copy", out=yt[:], in_=p[:])
                    xo = xn.next()
                    for nh in range(2):
                        ns = slice(nh * 512, (nh + 1) * 512)
                        po = pp.next()
                        for fc in range(8):
                            k.op("pe", "matmul", out=po[:], lhsT=yt[:, fc, :], rhs=wo[:, fc, ns], start=(fc == 0),
                                 stop=(fc == 7), inc=(fc == 7))
                        tt = t2.next()
                        k.op("dve", "tensor_tensor", out=tt[:], in0=po[:], in1=G1[r][:, ns], op=ALU.mult)
                        k.op("dve", "tensor_tensor", out=xo[:, ns], in0=tt[:], in1=x[:, ns], op=ALU.add)
                    k.dma("pool", xres[tok, :], xo[:])
                k.barrier()

        def mlp_stage(l, hT, tiles, last):
            ngroups = [(0, 512), (512, 512), (1024, 512), (1536, 512)]
            if not last:
                ngroups.append((2048, 256))
            with ExitStack() as st:
                wf = Rot([k.sb([128, 8, 512], BF16, st) for _ in range(2)])
                rl = Rot([k.sb([128, 512], F32, st) for _ in range(2)])
                hb = Rot([k.sb([128, 512], BF16, st) for _ in range(3)])
                pp = Rot([k.ps([128, 512], F32, st) for _ in range(4)])
                for p_i in range(8):
                    w = wf.next()
                    k.dma("pool", w[:], w_ff1[l, :, p_i * 512:(p_i + 1) * 512].re("(kc p) n -> p kc n", p=128))
                    for j in range(4):
                        for (q0, n) in ngroups:
                            ps = pp.next()
                            for fc in range(8):
                                k.op("pe", "matmul", out=ps[:, 0:n], lhsT=w[:, fc, j * 128:(j + 1) * 128],
                                     rhs=hT.all()[:, fc, q0:q0 + n], start=(fc == 0), stop=(fc == 7), inc=(fc == 7))
                            r_ = rl.next()
                            k.op("act", "activation", out=r_[:, 0:n], in_=ps[:, 0:n], func=AF.Relu)
                            h_ = hb.next()
                            k.op("dve", "tensor_tensor", out=h_[:, 0:n], in0=r_[:, 0:n], in1=r_[:, 0:n], op=ALU.mult)
                            row = (p_i * 4 + j) * 128
                            k.dma("sp", hidT[row:row + 128, q0:q0 + n], h_[:, 0:n])
                k.barrier()
            with ExitStack() as st:
                w2 = k.sb([128, 32, D], BF16, st)
                for i in range(4):
                    k.dma("pool", w2[:, i * 8:(i + 1) * 8, :],
                          w_ff2[l, i * 1024:(i + 1) * 1024, :].re("(kc p) n -> p kc n", p=128))
                G2 = [k.sb([128, D], F32, st) for _ in range(2)]
                for r in range(2):
                    k.dma("sp", G2[r][:], mods[l, r, 5 * D:6 * D].pbc())
                if last:
                    fg = k.sb([128, D], F32, st)
                    k.dma("sp", fg[:], final_g[:].pbc())
                    sq = k.sb([128, D], F32, st)
                    ssr = Rot([k.sb([128, 1], F32, st) for _ in range(2)])
                hg = Rot([k.sb([128, 32, 128], BF16, st) for _ in range(3)])
                xt = Rot([k.sb([128, D], F32, st) for _ in range(2)])
                xn = Rot([k.sb([128, D], F32, st) for _ in range(2)])
                t2 = Rot([k.sb([128, 512], F32, st) for _ in range(2)])
                pp = Rot([k.ps([128, 512], F32, st) for _ in range(4)])
                for t in tiles:
                    r = 0 if t < NL else 1
                    tok = slice(t * 128, (t + 1) * 128)
                    hh = hg.next()
                    x = xt.next()
                    k.dma("sp", hh[:], hidT[:, tok].re("(nc n) q -> n nc q", n=128))
                    k.dma("sp", x[:], xres[tok, :])
                    xo = xn.next()
                    for nh in range(2):
                        ns = slice(nh * 512, (nh + 1) * 512)
                        po = pp.next()
                        for c_ in range(32):
                            k.op("pe", "matmul", out=po[:], lhsT=hh[:, c_, :], rhs=w2[:, c_, ns], start=(c_ == 0),
                                 stop=(c_ == 31), inc=(c_ == 31))
                        tt = t2.next()
                        k.op("dve", "tensor_tensor", out=tt[:], in0=po[:], in1=G2[r][:, ns], op=ALU.mult)
                        k.op("dve", "tensor_tensor", out=xo[:, ns], in0=tt[:], in1=x[:, ns], op=ALU.add)
                    if last:
                        ss = ssr.next()
                        k.op("act", "activation", out=sq[:], in_=xo[:], func=AF.Square, accum_out=ss[:])
                        rsqrt_mean(ss[:], ss[:], D)
                        k.op("dve", "scalar_tensor_tensor", out=xo[:], in0=xo[:], scalar=ss[:], in1=fg[:],
                             op0=ALU.mult, op1=ALU.mult)
                        k.dma("pool", out[tok, :], xo[:])
                    else:
                        k.dma("pool", xres[tok, :], xo[:])
                k.barrier()

        hT = k.sb([128, 8, NTOK], BF16)
        for l in range(depth):
            last = l == depth - 1
            need_ctx = not last
            all_t = list(range(NT))
            out_t = all_t if need_ctx else list(range(NL))
            norm_stage(l, norm1_g, 0, 1, hT, all_t)
            proj_stage(l, hT, all_t)
            attn_stage(l, need_ctx)
            ret_stage(l, need_ctx)
            merge_stage(l, out_t)
            norm_stage(l, norm2_g, 3, 4, hT, out_t)
            mlp_stage(l, hT, out_t, last)
        if dbg:
            xd = k.dram("dbg_x", [NTOK, D], F32, kind="ExternalOutput")
            k.dma("sp", xd[:, :], xres[:, :])
            for nm, src in (("AqT", AqT), ("AkT", AkT), ("Av", Av), ("BqT", BqT), ("CqT", CqT), ("CkT", CkT),
                            ("Ck", Ck), ("Cv", Cv), ("Cg", Cg), ("Gt", Gt), ("yaT", yaT), ("ybT", ybT),
                            ("ycT", ycT)):
                shp = list(src.ap.shape)
                dd = k.dram("dbg_" + nm, shp, BF16, kind="ExternalOutput")
                k.dma("sp", V(dd.ap, []), V(src.ap, []))
            md = k.dram("dbg_mods", [DEPTH, 2, 6 * D], F32, kind="ExternalOutput")
            k.dma("sp", V(md.ap, []), V(mods.ap, []))
        k.barrier()
        build.ninstr = k.ninstr
    return nc


def _consts():
    L, GW = 2048, 64
    row = np.repeat(np.arange(L // GW), GW).astype(np.float32)
    col = np.tile(np.arange(GW), L // GW).astype(np.float32)

    def tab(dim):
        n = dim // 4
        inv = (10000.0 ** (-np.arange(n, dtype=np.float32) / n)).astype(np.float32)
        ang = np.concatenate([row[:, None] * inv, col[:, None] * inv], axis=-1).astype(np.float32)
        cos, sin = np.cos(ang).astype(np.float32), np.sin(ang).astype(np.float32)
        return np.stack([np.concatenate([cos, cos], -1), np.concatenate([-sin, sin], -1)], axis=1)

    s = np.arange(128)[:, None]
    a = np.arange(128)[None, :]
    mask = np.stack([(a <= s), (s <= a)], axis=1).astype(np.float32)
    q = a
    ret = np.stack([np.maximum(q - s, 0), np.maximum(s - q, 0), (q >= s), (s >= q)], axis=1).astype(np.float32)
    p = np.arange(128, dtype=np.float32)
    ecol = np.stack([127 - p, p, p + 1, 128 - p], axis=1).astype(np.float32)
    return dict(c_ident=np.eye(128, dtype=np.float32), c_ropeh=tab(128), c_roper=tab(256), c_mask=mask,
                c_ret=ret, c_ecol=ecol)


def make_in_maps(inputs):
    f = lambda a: np.ascontiguousarray(np.asarray(a, dtype=np.float32))
    shared = {n: f(inputs[n]) for n in ("w_ada", "b_ada", "norm1_g", "norm2_g", "w_in", "qn_g", "kn_g", "sink",
                                         "ret_norm_g", "w_pa", "w_pb", "w_pc", "w_o", "w_ff1", "w_ff2", "final_g")}
    shared["ret_logit"] = f(inputs["ret_logit"]).reshape(DEPTH, 8)
    shared.update(_consts())
    x, c, ctx, c_ctx = f(inputs["x"]), f(inputs["c"]), f(inputs["ctx"]), f(inputs["c_ctx"])
    maps = []
    for b in range(8):
        m = dict(shared)
        m["x_in"] = x[b]
        m["ctx_in"] = ctx[b]
        cc = np.stack([c[b].reshape(8, 128).T, c_ctx.reshape(8, 128).T], axis=-1)
        m["ccol"] = np.ascontiguousarray(cc)
        maps.append(m)
    return maps


_NC = {}


def kernel(**inputs):
    if "nc" not in _NC:
        _NC["nc"] = build()
    nc = _NC["nc"]
    maps = make_in_maps(inputs)
    res = run_bass_kernel_spmd(nc, maps, core_ids=list(range(8)))
    return np.stack([np.asarray(r["out"], dtype=np.float32) for r in res.results], axis=0)
```
